# Optimizing a Trainium2 kernel written in Bass

```python
import numpy as np
import jax
import jax.numpy as jnp
from jax import lax


D_MODEL = 1024
BATCH = 2
SEQ = 16384
DEPTH = 2
DEC_BATCH = 32
DEC_SEQ = 2048
PAST_LEN = 128

HEAD_DIM = 64
ATTN_SCALE = HEAD_DIM ** -0.5
A_HEADS = D_MODEL // 2 // HEAD_DIM
A_WIDTH = A_HEADS * HEAD_DIM
A_BRANCHES = ((128, 1), (512, 4), (2048, 16))
A_BLOCK = 128
B_GROUPS = 4
B_WINDOWS = (2, 4, 8, 16)
B_WIDTH = D_MODEL // 2
B_GROUP_DIM = B_WIDTH // B_GROUPS
EVEN_IN = 3 * A_WIDTH + B_WIDTH
EVEN_MIX = A_WIDTH + B_WIDTH
C_HEADS = D_MODEL // HEAD_DIM
C_WIDTH = C_HEADS * HEAD_DIM
GRID_W = 64
NA_ROWS = 8
NA_COLS = 16
T5_BUCKETS = 32
T5_MAX_DIST = 1024
D_FF = -(-8 * D_MODEL // (3 * 256)) * 256
N_EVEN = (DEPTH + 1) // 2
N_ODD = DEPTH // 2
NEG = -1e30
EPS = 1e-6

kernel_name = 'hybrid_dilated_pool_natten_encoder'


def rms_norm(x, g):
    xf = x.astype(jnp.float32)
    y = xf * lax.rsqrt(jnp.mean(xf * xf, axis=-1, keepdims=True) + EPS)
    return (y * g.astype(jnp.float32)).astype(x.dtype)


def modulate(h, shift, scale):
    return h * (1 + scale[:, None, :]) + shift[:, None, :]


def t5_bucket(rel):
    nb = T5_BUCKETS // 2
    max_exact = nb // 2
    ret = (rel > 0).astype(np.int32) * nb
    n = np.abs(rel)
    large = max_exact + (np.log(np.maximum(n, 1) / max_exact) / np.log(T5_MAX_DIST / max_exact)
                         * (nb - max_exact)).astype(np.int32)
    large = np.minimum(large, nb - 1)
    return (ret + np.where(n < max_exact, n, large)).astype(np.int32)


def dilated_branch(q, k, v, t5_table, window, dilation):
    b, L, h, dh = q.shape
    r = (window // 2) // dilation
    Ls = L // dilation

    def to_sub(t):
        return t.reshape(b, Ls, dilation, h, dh).transpose(0, 2, 1, 3, 4).reshape(b * dilation, Ls, h, dh)

    qs, ks, vs = to_sub(q), to_sub(k), to_sub(v)
    bq = min(A_BLOCK, Ls)
    nb = -(-Ls // bq)
    Lp = nb * bq
    kw = bq + 2 * r
    qs = jnp.pad(qs, ((0, 0), (0, Lp - Ls), (0, 0), (0, 0)))
    pad_k = ((0, 0), (r, Lp - Ls + r), (0, 0), (0, 0))
    ks = jnp.pad(ks, pad_k)
    vs = jnp.pad(vs, pad_k)
    idx = np.arange(nb)[:, None] * bq + np.arange(kw)[None, :]
    kb = ks[:, idx]
    vb = vs[:, idx]
    qb = qs.reshape(-1, nb, bq, h, dh)
    rel = np.arange(kw)[None, :] - r - np.arange(bq)[:, None]
    kpos = idx - r
    valid = (np.abs(rel) <= r)[None] & ((kpos >= 0) & (kpos < Ls))[:, None, :]
    bias = jnp.transpose(t5_table[t5_bucket(rel * dilation)], (2, 0, 1)).astype(jnp.float32)
    s = jnp.einsum('nbqhd,nbkhd->nbhqk', qb, kb, preferred_element_type=jnp.float32) * ATTN_SCALE + bias
    s = jnp.where(valid[None, :, None], s, NEG)
    m = jnp.max(s, axis=-1)
    p = jnp.exp(s - m[..., None])
    den = jnp.sum(p, axis=-1)
    o = jnp.einsum('nbhqk,nbkhd->nbqhd', p.astype(vb.dtype), vb, preferred_element_type=jnp.float32)
    o = o / jnp.transpose(den, (0, 1, 3, 2))[..., None]
    lse = jnp.transpose(m + jnp.log(den), (0, 1, 3, 2))
    o = o.reshape(b, dilation, Lp, h, dh)[:, :, :Ls].transpose(0, 2, 1, 3, 4).reshape(b, L, h, dh)
    lse = lse.reshape(b, dilation, Lp, h)[:, :, :Ls].transpose(0, 2, 1, 3).reshape(b, L, h)
    return o, lse


def multiscale_pool(u, w_grp, scale):
    b, L, _ = u.shape
    ug = u.reshape(b, L, B_GROUPS, B_GROUP_DIM)
    t = np.arange(L)
    outs = []
    for g, w in enumerate(B_WINDOWS):
        ch = ug[:, :, g].astype(jnp.float32)
        cs = jnp.concatenate([jnp.zeros((b, 1, B_GROUP_DIM), jnp.float32), jnp.cumsum(ch, axis=1)], axis=1)
        lo = np.clip(t - w // 2, 0, L)
        hi = np.clip(t + w // 2, 0, L)
        cnt = (hi - lo).astype(np.float32)
        mean = (cs[:, hi] - cs[:, lo]) / cnt[None, :, None]
        outs.append(mean - ch)
    pooled = jnp.stack(outs, axis=2)
    y = jnp.einsum('blgc,gcd->blgd', pooled, w_grp.astype(jnp.float32))
    y = y * scale.astype(jnp.float32).reshape(B_GROUPS, B_GROUP_DIM)
    return y.reshape(b, L, B_WIDTH).astype(u.dtype)


def even_mixer(h, w_in, w_out, pool_w, pool_scale, t5_table):
    b, L, _ = h.shape
    z = h @ w_in
    q, k, v, u = jnp.split(z, [A_WIDTH, 2 * A_WIDTH, 3 * A_WIDTH], axis=-1)
    q = q.reshape(b, L, A_HEADS, HEAD_DIM)
    k = k.reshape(b, L, A_HEADS, HEAD_DIM)
    v = v.reshape(b, L, A_HEADS, HEAD_DIM)
    outs, lses = [], []
    for window, dilation in A_BRANCHES:
        o, l = dilated_branch(q, k, v, t5_table, window, dilation)
        outs.append(o)
        lses.append(l)
    wts = jax.nn.softmax(jnp.stack(lses, axis=0), axis=0)
    ya = jnp.sum(wts[..., None] * jnp.stack(outs, axis=0), axis=0)
    ya = ya.reshape(b, L, A_WIDTH).astype(h.dtype)
    yb = multiscale_pool(u, pool_w, pool_scale)
    return jnp.concatenate([ya, yb], axis=-1) @ w_out


def neighbourhood_attn(q, k, v, rpb):
    b, L, h, dh = q.shape
    rows = L // GRID_W
    kh = min(NA_ROWS, rows)
    kw = NA_COLS
    qg = q.reshape(b, rows, GRID_W, h, dh)
    kg = k.reshape(b, rows, GRID_W, h, dh)
    vg = v.reshape(b, rows, GRID_W, h, dh)
    c = np.arange(GRID_W)
    cstart = np.clip(c - kw // 2, 0, GRID_W - kw)
    colmask = (c[None, :] >= cstart[:, None]) & (c[None, :] < cstart[:, None] + kw)
    coloff = np.clip(c[None, :] - c[:, None] + NA_COLS - 1, 0, 2 * NA_COLS - 2)

    def row_block(r):
        rs = jnp.clip(r - kh // 2, 0, rows - kh)
        qr = lax.dynamic_index_in_dim(qg, r, axis=1, keepdims=False)
        kr = lax.dynamic_slice_in_dim(kg, rs, kh, axis=1)
        vr = lax.dynamic_slice_in_dim(vg, rs, kh, axis=1)
        rowoff = rs + jnp.arange(kh) - r + (NA_ROWS - 1)
        bias = jnp.take(rpb, rowoff, axis=1)[:, :, coloff]
        bias = jnp.transpose(bias, (0, 2, 1, 3)).astype(jnp.float32)
        s = jnp.einsum('bqhd,bikhd->bhqik', qr, kr, preferred_element_type=jnp.float32) * ATTN_SCALE + bias
        s = jnp.where(colmask[:, None, :], s, NEG)
        p = jax.nn.softmax(s.reshape(b, h, GRID_W, kh * GRID_W), axis=-1)
        return jnp.einsum('bhqn,bnhd->bqhd', p.astype(vr.dtype), vr.reshape(b, kh * GRID_W, h, dh))

    out = lax.map(row_block, jnp.arange(rows))
    return jnp.transpose(out, (1, 0, 2, 3, 4)).reshape(b, L, h * dh)


def odd_mixer(h, w_qkv, w_out, rpb):
    b, L, _ = h.shape
    q, k, v = jnp.split(h @ w_qkv, 3, axis=-1)
    q = q.reshape(b, L, C_HEADS, HEAD_DIM)
    k = k.reshape(b, L, C_HEADS, HEAD_DIM)
    v = v.reshape(b, L, C_HEADS, HEAD_DIM)
    return neighbourhood_attn(q, k, v, rpb).astype(h.dtype) @ w_out


def swiglu(h, w_gate, w_up, w_down):
    return (jax.nn.silu(h @ w_gate) * (h @ w_up)) @ w_down


def setup_inputs(seed: int = 0) -> dict:
    key = jax.random.key(seed)
    ks = jax.random.split(key, 20)
    f32 = jnp.float32

    def nrm(k, shape, s):
        return jax.random.normal(k, shape, f32) * s

    return {
        'x_prompt': nrm(ks[0], (BATCH, SEQ, D_MODEL), 1.0),
        'x_sample': nrm(ks[1], (DEC_BATCH, DEC_SEQ, D_MODEL), 1.0),
        'c_prompt': nrm(ks[2], (BATCH, D_MODEL), 1.0),
        'c_sample': nrm(ks[3], (DEC_BATCH, D_MODEL), 1.0),
        'norm_g': 1.0 + nrm(ks[4], (DEPTH, 4, D_MODEL), 0.05),
        'ada_w': nrm(ks[5], (DEPTH, D_MODEL, 6 * D_MODEL), 0.5 * D_MODEL ** -0.5),
        'ada_b': nrm(ks[6], (DEPTH, 6 * D_MODEL), 0.02),
        'ffn_w_gate': nrm(ks[7], (DEPTH, D_MODEL, D_FF), D_MODEL ** -0.5),
        'ffn_w_up': nrm(ks[8], (DEPTH, D_MODEL, D_FF), D_MODEL ** -0.5),
        'ffn_w_down': nrm(ks[9], (DEPTH, D_FF, D_MODEL), D_FF ** -0.5),
        'even_w_in': nrm(ks[10], (N_EVEN, D_MODEL, EVEN_IN), D_MODEL ** -0.5),
        'even_w_out': nrm(ks[11], (N_EVEN, EVEN_MIX, D_MODEL), EVEN_MIX ** -0.5),
        'pool_w': nrm(ks[12], (N_EVEN, B_GROUPS, B_GROUP_DIM, B_GROUP_DIM), B_GROUP_DIM ** -0.5),
        'pool_scale': 1.0 + nrm(ks[13], (N_EVEN, B_WIDTH), 0.1),
        't5_table': nrm(ks[14], (T5_BUCKETS, A_HEADS), 0.5),
        'odd_w_qkv': nrm(ks[15], (N_ODD, D_MODEL, 3 * C_WIDTH), D_MODEL ** -0.5),
        'odd_w_out': nrm(ks[16], (N_ODD, C_WIDTH, D_MODEL), C_WIDTH ** -0.5),
        'odd_rpb': nrm(ks[17], (N_ODD, C_HEADS, 2 * NA_ROWS - 1, 2 * NA_COLS - 1), 0.5),
    }


def reference(x_prompt, x_sample, c_prompt, c_sample, norm_g, ada_w, ada_b, ffn_w_gate, ffn_w_up,
              ffn_w_down, even_w_in, even_w_out, pool_w, pool_scale, t5_table, odd_w_qkv, odd_w_out, odd_rpb):
    def trunk(x, c):
        for layer in range(DEPTH):
            mod = jax.nn.silu(c) @ ada_w[layer] + ada_b[layer]
            sh1, sc1, g1, sh2, sc2, g2 = jnp.split(mod, 6, axis=-1)
            g = norm_g[layer]
            h = modulate(rms_norm(x, g[0]), sh1, sc1)
            i = layer // 2
            if layer % 2 == 0:
                y = even_mixer(h, even_w_in[i], even_w_out[i], pool_w[i], pool_scale[i], t5_table)
            else:
                y = odd_mixer(h, odd_w_qkv[i], odd_w_out[i], odd_rpb[i])
            x = x + g1[:, None, :] * rms_norm(y, g[1])
            h = modulate(rms_norm(x, g[2]), sh2, sc2)
            y = swiglu(h, ffn_w_gate[layer], ffn_w_up[layer], ffn_w_down[layer])
            x = x + g2[:, None, :] * rms_norm(y, g[3])
        return x

    y_prompt = trunk(x_prompt, c_prompt)
    y_sample = trunk(x_sample, c_sample)
    return (y_prompt, y_sample)
```

```python
import numpy as np
import ml_dtypes
from contextlib import ExitStack
import concourse.bass as bass
import concourse.mybir as mybir
from concourse.bass_utils import run_bass_kernel_spmd

F32 = mybir.dt.float32
BF16 = mybir.dt.bfloat16
AF = mybir.ActivationFunctionType
ALU = mybir.AluOpType
NPBF = ml_dtypes.bfloat16

D = 1024
DFF = 2816
NF = 22
SEQ_S = 2048
HALO0 = 1280
PBUF = 4096 + 2 * HALO0
X1P = 4608
EPS = 1e-6
DILS = (1, 4, 16)

ENGS = ("pe", "act", "dve", "pool", "sp")


class Buf:
    __slots__ = ("name", "w", "r")

    def __init__(self, name):
        self.name = name
        self.w = None
        self.r = {}


class Prog:
    def __init__(self):
        self.q = {e: [] for e in ENGS}
        self.cnt = {}
        self.known = {e: {} for e in ENGS}
        self.nops = 0

    def _deps(self, eng, reads, writes):
        deps = {}
        for b in reads:
            if b.w is not None:
                s, v = b.w
                if deps.get(s, 0) < v:
                    deps[s] = v
        for b in writes:
            if b.w is not None:
                s, v = b.w
                if deps.get(s, 0) < v:
                    deps[s] = v
            for s, v in b.r.items():
                if deps.get(s, 0) < v:
                    deps[s] = v
        out = []
        kn = self.known[eng]
        for s, v in deps.items():
            if eng == "pe" and s == "c_pe":
                continue
            if kn.get(s, 0) >= v:
                continue
            kn[s] = v
            out.append((s, v))
        return out

    def _post(self, tok, reads, writes):
        s, v = tok
        for b in reads:
            if b.r.get(s, 0) < v:
                b.r[s] = v
        for b in writes:
            b.w = tok
            b.r = {}

    @staticmethod
    def _flat(x):
        out = []
        for b in x:
            if isinstance(b, (list, tuple)):
                out.extend(Prog._flat(b))
            else:
                out.append(b)
        return out

    def op(self, eng, fn, reads=(), writes=()):
        reads, writes = self._flat(reads), self._flat(writes)
        waits = self._deps(eng, reads, writes)
        s = "c_" + eng
        self.cnt[s] = self.cnt.get(s, 0) + 1
        tok = (s, self.cnt[s])
        self.q[eng].append((waits, fn, s, 1))
        self._post(tok, reads, writes)
        self.nops += 1

    def dma(self, fn, sem, reads=(), writes=(), eng="sp"):
        reads, writes = self._flat(reads), self._flat(writes)
        waits = self._deps(eng, reads, writes)
        s = "d_" + sem
        self.cnt[s] = self.cnt.get(s, 0) + 16
        tok = (s, self.cnt[s])
        self.q[eng].append((waits, fn, s, 16))
        self._post(tok, reads, writes)
        self.nops += 1

    def barrier(self):
        for e in ENGS:
            waits = []
            kn = self.known[e]
            for s, v in self.cnt.items():
                if kn.get(s, 0) < v:
                    kn[s] = v
                    waits.append((s, v))
            if waits:
                self.q[e].append((waits, None, None, 0))

    def final_wait(self, eng="sp"):
        waits = []
        kn = self.known[eng]
        for s, v in self.cnt.items():
            if kn.get(s, 0) < v:
                kn[s] = v
                waits.append((s, v))
        self.q[eng].append((waits, None, None, 0))

    def emit(self, nc):
        names = sorted(self.cnt.keys())
        with ExitStack() as es:
            sems = {n: es.enter_context(nc.semaphore(n)) for n in names}
            block = es.enter_context(nc.Block())

            def run(engh, lst):
                for waits, fn, s, inc in lst:
                    for ws, wv in waits:
                        engh.wait_ge(sems[ws], wv)
                    if fn is not None:
                        ins = fn(engh)
                        ins.then_inc(sems[s], inc)

            @block.tensor
            def _(t):
                run(t, self.q["pe"])

            @block.scalar
            def _(t):
                run(t, self.q["act"])

            @block.vector
            def _(t):
                run(t, self.q["dve"])

            @block.gpsimd
            def _(t):
                run(t, self.q["pool"])

            @block.sync
            def _(t):
                run(t, self.q["sp"])


class Arena:
    def __init__(self, ap, nbytes):
        self.ap = ap
        self.cap = nbytes
        self.top = 0
        self.hi = nbytes
        self.peak = 0

    def alloc(self, shape_free, dtype, high=False):
        n = 1
        for s in shape_free:
            n *= s
        esz = 4 if dtype == F32 else 2
        nb = n * esz
        if high:
            self.hi -= (nb + 31) // 32 * 32
            start = self.hi
        else:
            start = self.top
            self.top += (nb + 31) // 32 * 32
        assert self.top <= self.hi, f"arena overflow {self.top} > {self.hi}"
        self.peak = max(self.peak, self.top + (self.cap - self.hi))
        v = self.ap[:, start // 2:(start + nb) // 2]
        if dtype == F32:
            v = v.bitcast(F32)
        if len(shape_free) > 1:
            names = [chr(ord("a") + i) for i in range(len(shape_free))]
            pat = "p (" + " ".join(names) + ") -> p " + " ".join(names)
            kw = {nm: s for nm, s in zip(names, shape_free)}
            v = v.rearrange(pat, **kw)
        return v


class Rot:
    def __init__(self, items):
        self.items = items
        self.i = 0

    def next(self):
        it = self.items[self.i % len(self.items)]
        self.i += 1
        return it


def _t5_bucket(rel):
    nb = 16
    max_exact = 8
    ret = (rel > 0).astype(np.int32) * nb
    n = np.abs(rel)
    large = max_exact + (np.log(np.maximum(n, 1) / max_exact) / np.log(1024 / max_exact)
                         * (nb - max_exact)).astype(np.int32)
    large = np.minimum(large, nb - 1)
    return (ret + np.where(n < max_exact, n, large)).astype(np.int32)


def _band_mats(kind):
    out = np.zeros((128, 4, 3, 128), np.float32)
    for g, w in enumerate((2, 4, 8, 16)):
        for t in range(128):
            lo = t - w // 2
            hi = t + w // 2
            if kind == "first":
                lo = max(lo, 0)
            if kind == "last":
                hi = min(hi, 128)
            cnt = hi - lo
            for tin in range(lo, hi):
                nb = 1
                tl = tin
                if tin < 0:
                    nb, tl = 0, tin + 128
                elif tin >= 128:
                    nb, tl = 2, tin - 128
                out[tl, g, nb, t] += 1.0 / cnt
            out[t, g, 1, t] -= 1.0
    return out


def _colmask():
    m = np.zeros((128, 64), np.float32)
    for qc in range(64):
        cs = min(max(qc - 8, 0), 48)
        for kc in range(cs, cs + 16):
            m[kc, qc] = 1.0
            m[64 + kc, qc] = 1.0
    return m


def _mask_int():
    m = np.zeros((128, 14), np.float32)
    for ee in range(14):
        e = ee - 2
        if 1 <= e <= 8:
            m[0:64, ee] = 1.0
        if 2 <= e <= 9:
            m[64:128, ee] = 1.0
    return m


CF_IDENT = 0
CF_COLMASK = 128
CF_MINT = 192
CF_MSTART = 206
CF_MEND = 220
CF_UVALID = 234
CF_J128 = 276
CF_J2 = 404
CF_N = 532
CB_BINT = 0
CB_BFIRST = 1536
CB_BLAST = 3072
CB_BPS = 4608
CB_BPE = 6144
CBAND_N = 7680
CB_VONES0 = 0
CB_VONES1 = 192
CB_N = 232


def p0_geom(m):
    q0 = 1024 + 1536 * m
    kv0 = q0 - 1024
    return kv0, 3584, 1024, 1536


def l0_vblocks(KV, Q0, Q):
    blocks = []
    for br, d in enumerate(DILS):
        qa, qb = Q0 // d, (Q0 + Q) // d
        lsub = KV // d
        ka = max(qa - 64, 0)
        ke = min(qb + 64, lsub)
        for cr in range(d):
            kb = ka
            while kb < ke:
                M = min(128, ke - kb)
                blocks.append((br, cr, kb, M))
                kb += 128
    return blocks


def host_consts(core):
    p = core % 4
    cf = np.zeros((128, CF_N), np.float32)
    cf[:, CF_IDENT:CF_IDENT + 128] = np.eye(128, dtype=np.float32)
    cf[:, CF_COLMASK:CF_COLMASK + 64] = _colmask()
    cf[:, CF_J128:CF_J128 + 128] = np.eye(128, dtype=np.float32)[::-1]
    j64 = np.eye(64, dtype=np.float32)[::-1]
    cf[0:64, CF_J2:CF_J2 + 64] = j64
    cf[64:128, CF_J2 + 64:CF_J2 + 128] = j64
    mi = _mask_int()
    cf[:, CF_MINT:CF_MINT + 14] = mi
    cf[:, CF_MSTART:CF_MSTART + 14] = 1.0 if p == 0 else mi
    cf[:, CF_MEND:CF_MEND + 14] = 1.0 if p == 3 else mi
    pos = np.arange(PBUF) + 4096 * p - HALO0
    valid = ((pos >= 0) & (pos < 16384)).astype(np.float32)
    for m in range(3):
        kv0, KV, Q0, Q = p0_geom(m)
        for ub in range(14):
            tok = kv0 + Q0 - 128 + ub * 128
            cf[:, CF_UVALID + m * 14 + ub] = valid[tok]
    cb = np.zeros((128, CB_N), np.float32)
    cbd = np.zeros((128, CBAND_N), np.float32)
    bint = _band_mats("int")
    bfirst = _band_mats("first")
    blast = _band_mats("last")
    cbd[:, CB_BINT:CB_BINT + 1536] = bint.reshape(128, 1536)
    cbd[:, CB_BFIRST:CB_BFIRST + 1536] = bfirst.reshape(128, 1536)
    cbd[:, CB_BLAST:CB_BLAST + 1536] = blast.reshape(128, 1536)
    cbd[:, CB_BPS:CB_BPS + 1536] = (bfirst if p == 0 else bint).reshape(128, 1536)
    cbd[:, CB_BPE:CB_BPE + 1536] = (blast if p == 3 else bint).reshape(128, 1536)
    for m in range(3):
        kv0, KV, Q0, Q = p0_geom(m)
        for i, (br, cr, kb, M) in enumerate(l0_vblocks(KV, Q0, Q)):
            d = DILS[br]
            toks = kv0 + cr + d * (kb + np.arange(M))
            cb[:M, CB_VONES0 + m * 61 + i] = valid[toks]
    for n in range(2):
        for b in range(20):
            tok = 1024 + 2048 * n + b * 128
            cb[:, CB_VONES1 + n * 20 + b] = valid[tok]
    vrow = np.broadcast_to(valid[None, :], (128, PBUF)).astype(NPBF)
    return cf, cb.astype(NPBF), np.ascontiguousarray(vrow), cbd.astype(NPBF)


def build_program(n_sample=4, do_prompt=True, do_layer1=True, taps=()):
    nc = bass.Bass("TRN2", target_bir_lowering=False)
    P = Prog()
    dt = nc.dram_tensor

    def din(name, shape, dtype=F32):
        return dt(name, list(shape), dtype, kind="ExternalInput").ap()

    def dscr(name, shape, dtype):
        return dt(name, list(shape), dtype, kind="Internal").ap()

    xs = din("xs", [4, SEQ_S, D])
    xp = din("xp", [PBUF, D])
    prm = din("prm", [256, 128])
    adaw = din("adaw", [2, D, 6 * D])
    wgate = din("wgate", [2, D, DFF])
    wup = din("wup", [2, D, DFF])
    wdown = din("wdown", [2, DFF, D])
    win = din("win", [D, 2048])
    wout0 = din("wout0", [D, D])
    poolw = din("poolw", [4, 128, 128])
    wqkv = din("wqkv", [D, 3072])
    wout1 = din("wout1", [D, D])
    t5pad_t = dt("t5pad", [3 * 8 * 384], F32, kind="ExternalInput").ap().tensor
    rpbpad_t = dt("rpbpad", [16 * 15 * 128], F32, kind="ExternalInput").ap().tensor
    cstf = din("cstf", [128, CF_N])
    cstb = din("cstb", [128, CB_N], BF16)
    cbands = din("cbands", [128, CBAND_N], BF16)
    vrow0 = din("vrow0", [128, PBUF], BF16)
    ys = dt("ys", [4, SEQ_S, D], F32, kind="ExternalOutput").ap()
    yp = dt("yp", [4096, D], F32, kind="ExternalOutput").ap()

    win_s = dscr("win_s", [4, 128, 4096], BF16)
    wout_s = dscr("wout_s", [2, 128, 8192], BF16)
    wgu_s = dscr("wgu_s", [2, 11, 128, 4096], BF16)
    wd_s = dscr("wd_s", [2, 2, 11, 128, 1024], BF16)
    wqkv_s = dscr("wqkv_s", [6, 128, 4096], BF16)
    xT_s = dscr("xT_s", [8, 128, 2048], F32)
    x1s_s = dscr("x1s_s", [4, 8, 128, SEQ_S], F32)
    h1s_s = dscr("h1s_s", [4, 8, 128, SEQ_S], BF16)
    x1p_s = dscr("x1p_s", [8, 128, X1P], F32)
    h1p_s = dscr("h1p_s", [8, 128, X1P], BF16)

    tap_out = {}
    for name, shape, dtype in taps:
        tap_out[name] = dt("tap_" + name, list(shape), dtype, kind="ExternalOutput").ap()

    es = ExitStack()
    ARENA_BYTES = 212000
    arena_t = es.enter_context(nc.sbuf_tensor("arena", [128, ARENA_BYTES // 2], BF16))
    ps_t = es.enter_context(nc.psum_tensor("ps", [128, 8, 512], F32))
    A = Arena(arena_t, ARENA_BYTES)
    bankbuf = [Buf(f"bank{i}") for i in range(8)]

    def bank(i):
        return ps_t[:, i, :]

    def bankpool(ids):
        return Rot([(bank(i), bankbuf[i]) for i in ids])

    cf = A.alloc([CF_N], F32)
    cbt = A.alloc([CB_N], BF16)
    PT = A.alloc([256], F32)
    MODV = A.alloc([2, 6, 8, 5], F32)
    identb = A.alloc([128], BF16)
    onesb = A.alloc([128], BF16)
    epsb = A.alloc([1], F32)
    tinyb = A.alloc([1], F32)
    poolw_b = A.alloc([4, 128], BF16)
    B_consts = Buf("consts")
    B_PT = Buf("PT")
    B_MODV = Buf("MODV")
    ident = cf[:, CF_IDENT:CF_IDENT + 128]
    colmask = cf[:, CF_COLMASK:CF_COLMASK + 64]

    band_tile = [None]

    def band(base, g, nb):
        o = base + (g * 3 + nb) * 128
        return band_tile[0][:, o:o + 128]

    P.dma(lambda e: e.dma_start(out=cf, in_=cstf), "consts", writes=[B_consts])
    P.dma(lambda e: e.dma_start(out=cbt, in_=cstb), "consts", writes=[B_consts])
    P.op("pool", lambda e: e.memset(onesb, 1.0), writes=[B_consts])
    P.op("pool", lambda e: e.memset(epsb, EPS), writes=[B_consts])
    P.op("pool", lambda e: e.memset(tinyb, 1e-30), writes=[B_consts])
    P.op("dve", lambda e: e.tensor_copy(out=identb, in_=ident), reads=[B_consts], writes=[B_consts])
    P.barrier()
    MARK0 = A.top

    prm_t = A.alloc([2, 128], F32)
    SC = A.alloc([40], F32)
    modT = A.alloc([2, 48, 5], F32)
    B_prm = Buf("prm")
    B_SC = Buf("SC")
    B_modT = Buf("modT")
    P.dma(lambda e: e.dma_start(out=prm_t, in_=prm.rearrange("(a p) n -> p a n", p=128)), "prm", writes=[B_prm])

    def f(e):
        e.transpose(out=bank(0)[:, 0:128], in_=prm_t[:, 0, :], identity=ident)
        return e.transpose(out=bank(0)[:, 128:256], in_=prm_t[:, 1, :], identity=ident)
    P.op("pe", f, reads=[B_prm, B_consts], writes=[bankbuf[0]])
    P.op("dve", lambda e: e.tensor_copy(out=PT, in_=bank(0)[:, 0:256]), reads=[bankbuf[0]], writes=[B_PT])
    P.op("act", lambda e: e.activation(out=SC, in_=PT[:, 0:40], func=AF.Silu), reads=[B_PT], writes=[B_SC])
    aw_slots = Rot([(A.alloc([8, 512], F32), Buf(f"aw{i}"), f"aw{i}") for i in range(3)])
    mod_steps = []
    def mod_tile(l, nt):
        awt, awb, aws = aw_slots.next()
        P.dma(lambda e: e.dma_start(out=awt, in_=adaw[l, :, nt * 512:(nt + 1) * 512].rearrange("(kc p) n -> p kc n", p=128)),
              aws, writes=[awb])
        for mm in range(4):
            jm = nt * 4 + mm

            def f(e, mm=mm, jm=jm):
                for kc in range(8):
                    ins = e.matmul(bank(1 + l)[:, jm * 5:jm * 5 + 5], lhsT=awt[:, kc, mm * 128:(mm + 1) * 128],
                                   rhs=SC[:, kc * 5:(kc + 1) * 5], start=(kc == 0), stop=(kc == 7))
                return ins
            P.op("pe", f, reads=[awb, B_SC], writes=[bankbuf[1 + l]])

    def mod_fin(l):
        P.op("dve", lambda e: e.tensor_tensor(
            out=modT[:, l, :, :], in0=bank(1 + l)[:, 0:240].rearrange("p (a b) -> p a b", b=5),
            in1=PT[:, 104 + 48 * l:104 + 48 * (l + 1)].unsqueeze(2).broadcast_to([128, 48, 5]), op=ALU.add),
            reads=[bankbuf[1 + l], B_PT], writes=[B_modT])
    for l in range(2):
        for nt in range(12):
            mod_steps.append(lambda l=l, nt=nt: mod_tile(l, nt))
        mod_steps.append(lambda l=l: mod_fin(l))

    def derived_vectors():
        _derived()

    def _derived():
        for l in range(2):
            def ng(j, l=l):
                o = 40 + (l * 4 + j) * 8
                return PT[:, o:o + 8].unsqueeze(2).broadcast_to([128, 8, 5])

            def md(j, l=l):
                return modT[:, l, j * 8:(j + 1) * 8, :]
            P.op("dve", lambda e, l=l, ng=ng, md=md: e.scalar_tensor_tensor(
                out=MODV[:, l, 0], in0=md(1), scalar=1.0, in1=ng(0), op0=ALU.add, op1=ALU.mult),
                reads=[B_modT, B_PT], writes=[B_MODV])
            P.op("dve", lambda e, l=l, md=md: e.tensor_copy(out=MODV[:, l, 1], in_=md(0)), reads=[B_modT], writes=[B_MODV])
            P.op("dve", lambda e, l=l, ng=ng, md=md: e.tensor_tensor(out=MODV[:, l, 2], in0=md(2), in1=ng(1), op=ALU.mult),
                 reads=[B_modT, B_PT], writes=[B_MODV])
            P.op("dve", lambda e, l=l, ng=ng, md=md: e.scalar_tensor_tensor(
                out=MODV[:, l, 3], in0=md(4), scalar=1.0, in1=ng(2), op0=ALU.add, op1=ALU.mult),
                reads=[B_modT, B_PT], writes=[B_MODV])
            P.op("dve", lambda e, l=l, md=md: e.tensor_copy(out=MODV[:, l, 4], in_=md(3)), reads=[B_modT], writes=[B_MODV])
            P.op("dve", lambda e, l=l, ng=ng, md=md: e.tensor_tensor(out=MODV[:, l, 5], in0=md(5), in1=ng(3), op=ALU.mult),
                 reads=[B_modT, B_PT], writes=[B_MODV])

    def mv(l, kind, kc, s):
        return MODV[:, l, kind, kc, s:s + 1]

    stg_f = Rot([(A.alloc([4096], F32), Buf(f"stgf{i}"), f"stgf{i}") for i in range(4)])
    stg_b = Rot([(A.alloc([4096], BF16), Buf(f"stgb{i}"), f"stgb{i}") for i in range(4)])
    cvt_steps = []
    cvt_i = [0]

    def _convert_now(loads, dst, nel, sview=None):
        sf, sfb, sfs = stg_f.next()
        sb, sbb, sbs = stg_b.next()
        for vf, src in loads:
            P.dma(lambda e, vf=vf, src=src, sf=sf: e.dma_start(out=vf(sf), in_=src), sfs, writes=[sfb])
        eng = "act" if cvt_i[0] % 2 == 0 else "dve"
        cvt_i[0] += 1
        if eng == "act":
            P.op("act", lambda e, sf=sf, sb=sb: e.activation(out=sb[:, 0:nel], in_=sf[:, 0:nel], func=AF.Copy),
                 reads=[sfb], writes=[sbb])
        else:
            P.op("dve", lambda e, sf=sf, sb=sb: e.tensor_copy(out=sb[:, 0:nel], in_=sf[:, 0:nel]),
                 reads=[sfb], writes=[sbb])
        src_v = sb[:, 0:nel] if sview is None else sview(sb[:, 0:nel])
        P.dma(lambda e, src_v=src_v, dst=dst: e.dma_start(out=dst, in_=src_v), sbs, reads=[sbb])

    def convert(*a, **k):
        cvt_steps.append(lambda: _convert_now(*a, **k))

    def full(sf):
        return sf

    def v3(a, b):
        return lambda sf: sf.rearrange("p (a b) -> p a b", a=a)

    sf, sfb, sfs = stg_f.next()
    P.dma(lambda e, sf=sf: e.dma_start(out=sf[:, 0:512].rearrange("p (g d) -> p g d", g=4),
                                        in_=poolw.rearrange("g c d -> c g d")), sfs, writes=[sfb])
    P.op("dve", lambda e, sf=sf: e.tensor_copy(out=poolw_b, in_=sf[:, 0:512].rearrange("p (g d) -> p g d", g=4)),
         reads=[sfb], writes=[B_consts])
    for g in range(4):
        convert([(v3(8, 512), win[:, g * 512:(g + 1) * 512].rearrange("(kc p) n -> p kc n", p=128))], win_s[g], 4096)
    for l, wo in enumerate((wout0, wout1)):
        for hf in range(2):
            convert([(v3(4, 1024), wo[hf * 512:(hf + 1) * 512, :].rearrange("(kc p) n -> p kc n", p=128))],
                    wout_s[l][:, hf * 4096:(hf + 1) * 4096], 4096)
    for l in range(2):
        for grp in range(11):
            lo = grp * 256
            convert([
                (lambda sf: sf.rearrange("p (kc t n) -> p kc t n", kc=8, t=2)[:, :, 0, :],
                 wgate[l, :, lo:lo + 256].rearrange("(kc p) n -> p kc n", p=128)),
                (lambda sf: sf.rearrange("p (kc t n) -> p kc t n", kc=8, t=2)[:, :, 1, :],
                 wup[l, :, lo:lo + 256].rearrange("(kc p) n -> p kc n", p=128)),
            ], wgu_s[l, grp], 4096)
        for mh in range(2):
            for g4 in range(6):
                ng_ = 2 if g4 < 5 else 1
                nfc = 2 * ng_
                f0 = g4 * 4
                convert([(lambda sf, nfc=nfc: sf[:, 0:nfc * 512].rearrange("p (f n) -> p f n", n=512),
                          wdown[l, f0 * 128:(f0 + nfc) * 128, mh * 512:(mh + 1) * 512].rearrange("(f p) n -> p f n", p=128))],
                        wd_s[l, mh, 2 * g4:2 * g4 + ng_].rearrange("g p n -> p g n"), nfc * 512,
                        sview=lambda v: v.rearrange("p (g n) -> p g n", n=1024))
    for g in range(6):
        convert([(v3(8, 512), wqkv[:, g * 512:(g + 1) * 512].rearrange("(kc p) n -> p kc n", p=128))], wqkv_s[g], 4096)
    mi = ci = 0
    while mi < len(mod_steps) or ci < len(cvt_steps):
        if mi < len(mod_steps):
            mod_steps[mi]()
            mi += 1
        for _ in range(2):
            if ci < len(cvt_steps):
                cvt_steps[ci]()
                ci += 1
    derived_vectors()
    P.barrier()
    A.top = MARK0

    def run_pipeline(items, lag):
        n = len(items)
        for i in range(n + lag):
            if i < n:
                items[i][0]()
            j = i - lag
            if j >= 0:
                items[j][1]()

    tap_done = set()

    def tap(name, src_ap, reads):
        if name in tap_out and name not in tap_done:
            tap_done.add(name)
            P.dma(lambda e: e.dma_start(out=tap_out[name], in_=src_ap), "tap", reads=reads)

    evac_i = [0]

    def evac_copy(out_ap, in_ap, reads, writes, eng=None):
        if eng is None:
            eng = "act" if evac_i[0] % 2 == 0 else "dve"
            evac_i[0] += 1
        if eng == "act":
            P.op("act", lambda e: e.activation(out=out_ap, in_=in_ap, func=AF.Copy), reads=reads, writes=writes)
        else:
            P.op("dve", lambda e: e.tensor_copy(out=out_ap, in_=in_ap), reads=reads, writes=writes)

    def squares(src_chunks, src_bufs, sq, B_sq, ntok=512):
        for c in range(8):
            P.op("act", lambda e, c=c: e.activation(out=sq[:, c, 0:ntok], in_=src_chunks(c), func=AF.Square),
                 reads=[src_bufs[c]], writes=[B_sq])

    def rms_rstd(sq, B_sq, ssbank, lnt, B_lnt, ntok=512):
        ssap, ssb = ssbank

        def f(e):
            for c in range(8):
                ins = e.matmul(ssap[:, 0:ntok], lhsT=onesb, rhs=sq[:, c, 0:ntok], start=(c == 0), stop=(c == 7))
            return ins
        P.op("pe", f, reads=[B_sq, B_consts], writes=[ssb])
        P.op("act", lambda e: e.activation(out=lnt[:, 0:ntok], in_=ssap[:, 0:ntok], func=AF.Ln, scale=1.0 / D, bias=epsb),
             reads=[ssb, B_consts], writes=[B_lnt])
        P.op("act", lambda e: e.activation(out=ssap[:, 0:ntok], in_=lnt[:, 0:ntok], func=AF.Exp, scale=-0.5),
             reads=[B_lnt], writes=[ssb])
        return ssap, ssb

    def norm_affine(src, B_src, rstd, B_rstd, tmps, dst, B_dst, l, kindA, s, ntok=512):
        for c in range(8):
            tmp, tb = tmps.next()
            P.op("dve", lambda e, c=c, tmp=tmp: e.tensor_tensor(out=tmp[:, 0:ntok], in0=src[:, c, 0:ntok], in1=rstd[:, 0:ntok], op=ALU.mult),
                 reads=[B_src[c], B_rstd], writes=[tb])
            P.op("act", lambda e, c=c, tmp=tmp: e.activation(out=dst[:, c, 0:ntok], in_=tmp[:, 0:ntok], func=AF.Identity,
                                                             scale=mv(l, kindA, c, s), bias=mv(l, kindA + 1, c, s)),
                 reads=[tb, B_MODV], writes=[B_dst])

    def resid_update(yc, B_yc, rstd, B_rstd, tmps, xT, B_x, ntok=512):
        for c in range(8):
            tmp, tb = tmps.next()
            P.op("dve", lambda e, c=c, tmp=tmp: e.tensor_tensor(out=tmp[:, 0:ntok], in0=yc[:, c, 0:ntok], in1=rstd[:, 0:ntok], op=ALU.mult),
                 reads=[B_yc[c], B_rstd], writes=[tb])
            P.op("pool", lambda e, c=c, tmp=tmp: e.tensor_tensor(out=xT[:, c, 0:ntok], in0=xT[:, c, 0:ntok], in1=tmp[:, 0:ntok], op=ALU.add),
                 reads=[tb, B_x[c]], writes=[B_x[c]])

    def evac_scaled_sq(bap, bb, yc_c, B_yc_c, sq_c, B_sq, l, kindC, c, s):
        P.op("act", lambda e: e.activation(out=sq_c, in_=bap, func=AF.Square), reads=[bb], writes=[B_sq])
        P.op("dve", lambda e: e.tensor_scalar(out=yc_c, in0=bap, scalar1=mv(l, kindC, c, s), scalar2=None, op0=ALU.mult),
             reads=[B_MODV], writes=[B_yc_c, bb])

    def phase_d(l, s, Q, mix_chunks, B_mix, x_src, x_src_buf, sink):
        m0 = A.top
        wo_sl = Rot([(A.alloc([8, 128], BF16), Buf(f"wo{i}"), f"wo{i}") for i in range(3)])
        xT_sl = [(A.alloc([8, 512], F32), [Buf(f"xT{i}_{c}") for c in range(8)], f"xT{i}") for i in range(2)]
        yTp = A.alloc([8, 512], F32)
        B_yTp = [Buf(f"yTp_{c}") for c in range(8)]
        yTd = A.alloc([8, 512], F32)
        B_yTd = [Buf(f"yTd_{c}") for c in range(8)]
        sqA = A.alloc([8, 512], BF16)
        B_sqA = Buf("sqA")
        sqB = A.alloc([8, 512], BF16)
        B_sqB = Buf("sqB")
        h2 = A.alloc([8, 512], BF16)
        B_h2 = Buf("h2")
        aT = A.alloc([NF, 512], BF16)
        B_aT = Buf("aT")
        sg_sl = Rot([(A.alloc([512], BF16), Buf(f"sg{i}")) for i in range(2)])
        tmps = Rot([(A.alloc([512], F32), Buf(f"tmp{i}")) for i in range(3)])
        lntA = A.alloc([512], F32)
        B_lntA = Buf("lntA")
        lntB, B_lntB = lntA, B_lntA
        wgu_sl = Rot([(A.alloc([8, 2, 256], BF16), Buf(f"wgu{i}"), f"wgu{i}") for i in range(3)])
        wgu_pref = {}
        wd_pref = {}

        def load_wgu(grp):
            wg, wgb, wgs = wgu_sl.next()
            P.dma(lambda e: e.dma_start(out=wg.rearrange("p a b c -> p (a b c)"), in_=wgu_s[l, grp]), wgs, writes=[wgb])
            return wg, wgb

        def load_wd(mh, grp):
            wd, wdb, wds = wd_sl.next()
            P.dma(lambda e: e.dma_start(out=wd.rearrange("p a b -> p (a b)"), in_=wd_s[l, mh, grp]), wds, writes=[wdb])
            return wd, wdb

        def prefetch_wgu(t):
            wgu_pref[t] = [load_wgu(0), load_wgu(1)]

        def prefetch_wd(t):
            wd_pref[t] = [load_wd(0, g_) for g_ in range(3)]
        wd_sl = Rot([(A.alloc([2, 512], BF16), Buf(f"wd{i}"), f"wd{i}") for i in range(4)])
        pb_mm = bankpool([0, 1, 2, 3])
        ssA = (bank(3), bankbuf[3])
        ssB = (bank(4), bankbuf[4])
        pb_dn = [(bank(i), bankbuf[i]) for i in (4, 5, 6, 7)]
        ctx = dict(sqB=sqB, B_sqB=B_sqB, lntB=lntB, B_lntB=B_lntB, ssB=ssB, tmps=tmps, pb_mm=pb_mm, yTd=yTd, B_yTd=B_yTd)
        ctx["extra"] = sink("alloc", None, None, None, ctx)
        ntiles = Q // 512
        wo_v = wout_s[l].rearrange("p (kc n) -> p kc n", kc=8)

        def P0(t):
            t0 = t * 512
            xT, B_x, xsem = xT_sl[t % 2]
            P.dma(lambda e: e.dma_start(out=xT, in_=x_src(t0)), xsem, reads=[x_src_buf], writes=[B_x])
            for m in range(8):
                wo, wob, wos = wo_sl.next()
                P.dma(lambda e, wo=wo, m=m: e.dma_start(out=wo, in_=wo_v[:, :, m * 128:(m + 1) * 128]), wos, writes=[wob])
                bap, bb = pb_mm.next()

                def f(e, bap=bap, wo=wo):
                    for kc in range(8):
                        ins = e.matmul(bap, lhsT=wo[:, kc, :], rhs=mix_chunks(kc, t0, 512), start=(kc == 0), stop=(kc == 7))
                    return ins
                P.op("pe", f, reads=[wob, B_mix], writes=[bb])
                evac_scaled_sq(bap, bb, yTp[:, m, :], B_yTp[m], sqA[:, m, :], B_sqA, l, 2, m, s)

        st_ = {}

        def P1a(t):
            st_["rA"] = rms_rstd(sqA, B_sqA, ssA, lntA, B_lntA)

        def P1b(t):
            xT, B_x, _ = xT_sl[t % 2]
            rstd, B_rstd = st_["rA"]
            resid_update(yTp, B_yTp, rstd, B_rstd, tmps, xT, B_x)

        def P1c(t):
            xT, B_x, _ = xT_sl[t % 2]
            squares(lambda c: xT[:, c, :], B_x, sqA, B_sqA)

        def P2a(t):
            st_["rA"] = rms_rstd(sqA, B_sqA, ssA, lntA, B_lntA)

        def P2b(t):
            xT, B_x, _ = xT_sl[t % 2]
            rstd, B_rstd = st_["rA"]
            norm_affine(xT, B_x, rstd, B_rstd, tmps, h2, B_h2, l, 3, s)

        def G(t, hooks):
            loaded = list(wgu_pref.pop(t))
            for grp in range(11):
                if grp + 2 < 11:
                    loaded.append(load_wgu(grp + 2))
                wg, wgb = loaded[grp]
                for j in range(2):
                    fch = grp * 2 + j
                    bg, bgb = pb_mm.next()
                    bu, bub = pb_mm.next()

                    def f(e, wg=wg, j=j, bg=bg, bu=bu):
                        for kc in range(8):
                            e.matmul(bg, lhsT=wg[:, kc, 0, j * 128:(j + 1) * 128], rhs=h2[:, kc, :], start=(kc == 0), stop=(kc == 7))
                        for kc in range(8):
                            ins = e.matmul(bu, lhsT=wg[:, kc, 1, j * 128:(j + 1) * 128], rhs=h2[:, kc, :], start=(kc == 0), stop=(kc == 7))
                        return ins
                    P.op("pe", f, reads=[wgb, B_h2], writes=[bgb, bub])
                    sg, sgb = sg_sl.next()
                    P.op("act", lambda e, sg=sg, bg=bg: e.activation(out=sg, in_=bg, func=AF.Silu), reads=[bgb], writes=[sgb])
                    P.op("dve", lambda e, sg=sg, bu=bu, fch=fch: e.tensor_tensor(out=aT[:, fch, :], in0=bu, in1=sg, op=ALU.mult),
                         reads=[sgb, bub], writes=[B_aT])
                for h in hooks.get(grp, ()):
                    h()

        def Dn(t, hooks):
            k = 0
            loaded = list(wd_pref.pop(t))
            order = [(mh, grp) for mh in range(2) for grp in range(11)]
            for mh in range(2):
                for grp in range(11):
                    if k + 3 < 22:
                        loaded.append(load_wd(*order[k + 3]))
                    wd, wdb = loaded[k]

                    def f(e, wd=wd, grp=grp):
                        for j in range(2):
                            for m in range(4):
                                ins = e.matmul(pb_dn[m][0], lhsT=wd[:, j, m * 128:(m + 1) * 128], rhs=aT[:, grp * 2 + j, :],
                                               start=(grp == 0 and j == 0), stop=(grp == 10 and j == 1))
                        return ins
                    P.op("pe", f, reads=[wdb, B_aT], writes=[pb_dn[m][1] for m in range(4)])
                    for h in hooks.get(k, ()):
                        h()
                    k += 1
                for m in range(4):
                    mm_ = mh * 4 + m
                    evac_scaled_sq(pb_dn[m][0], pb_dn[m][1], yTd[:, mm_, :], B_yTd[mm_], sqB[:, mm_, :], B_sqB, l, 5, mm_, s)

        def Q1a(t):
            st_["rB"] = rms_rstd(sqB, B_sqB, ssB, lntB, B_lntB)

        def Q1b(t):
            xT, B_x, _ = xT_sl[t % 2]
            rstd, B_rstd = st_["rB"]
            resid_update(yTd, B_yTd, rstd, B_rstd, tmps, xT, B_x)

        def Q1c(t):
            xT, B_x, _ = xT_sl[t % 2]
            sink("q1", t * 512, xT, B_x, ctx)

        def Q2a(t):
            xT, B_x, _ = xT_sl[t % 2]
            sink("q2a", t * 512, xT, B_x, ctx)

        def Q2b(t):
            xT, B_x, _ = xT_sl[t % 2]
            sink("q2b", t * 512, xT, B_x, ctx)

        prefetch_wgu(0)
        P0(0)
        P1a(0)
        P1b(0)
        P1c(0)
        P2a(0)
        P2b(0)
        for t in range(ntiles):
            gh, dh = {}, {}
            if t > 0:
                gh[0] = [lambda t=t: Q1a(t - 1)]
                gh[1] = [lambda t=t: Q1b(t - 1)]
                gh[2] = [lambda t=t: Q1c(t - 1)]
                gh[3] = [lambda t=t: Q2a(t - 1)]
                gh[4] = [lambda t=t: Q2b(t - 1)]
            gh[5] = [lambda t=t: prefetch_wd(t)]
            if t + 1 < ntiles:
                gh[7] = [lambda t=t: P0(t + 1)]
                dh[1] = [lambda t=t: P1a(t + 1)]
                dh[4] = [lambda t=t: P1b(t + 1)]
                dh[8] = [lambda t=t: P1c(t + 1)]
                dh[12] = [lambda t=t: P2a(t + 1)]
                dh[15] = [lambda t=t: P2b(t + 1)]
                dh[17] = [lambda t=t: prefetch_wgu(t + 1)]
            G(t, gh)
            Dn(t, dh)
        for fn in (Q1a, Q1b, Q1c, Q2a, Q2b):
            fn(ntiles - 1)
        P.barrier()
        A.top = m0

    def layer0_segment(kind, idx):
        if kind == "S":
            KV, Q0, Q = SEQ_S, 0, SEQ_S
            s = idx

            def x_rows(t0, n):
                return xs[idx, t0:t0 + n, :]
            x1_dst = x1s_s[idx]
            h1_dst = h1s_s[idx]
            dst_off = 0
        else:
            kv0, KV, Q0, Q = p0_geom(idx)
            s = 4

            def x_rows(t0, n):
                return xp[kv0 + t0:kv0 + t0 + n, :]
            x1_dst = x1p_s
            h1_dst = h1p_s
            dst_off = 1536 * idx
        NB_U = Q // 128 + 2
        m_seg = A.top
        kT = A.alloc([4, KV], BF16)
        vT = A.alloc([4, KV], BF16)
        qT = A.alloc([4, Q], BF16)
        utok = A.alloc([NB_U, 512], BF16)
        B_kT, B_vT, B_qT, B_u = Buf("kT"), Buf("vT"), Buf("qT"), Buf("utok")
        B_xTs = Buf("xT_s")
        m_a = A.top
        winr = A.alloc([4, 8, 512], BF16)
        B_win = Buf("winr")
        for g in range(4):
            P.dma(lambda e, g=g: e.dma_start(out=winr[:, g].rearrange("p a b -> p (a b)"), in_=win_s[g]), "winr", writes=[B_win])
        xtok_sl = Rot([(A.alloc([2, 1024], F32), Buf(f"xtok{i}"), f"xtok{i}") for i in range(2)])
        xT = A.alloc([8, 512], F32)
        B_x = [Buf(f"xTa_{c}") for c in range(8)]
        sq = A.alloc([8, 512], BF16)
        B_sq = Buf("sqa")
        lnt = A.alloc([512], F32)
        B_lnt = Buf("lnta")
        hT_sl = Rot([(A.alloc([8, 512], BF16), Buf(f"hT{i}")) for i in range(1)])
        tmps = Rot([(A.alloc([512], F32), Buf(f"tmpa{i}")) for i in range(3)])
        vr_sl = Rot([(A.alloc([512], BF16), Buf(f"vr{i}"), f"vr{i}") for i in range(2)])
        pb_tr = bankpool([0, 1])
        pb_ss = bankpool([2])
        pb_mm = bankpool([3, 4, 5, 6, 7])
        for t in range(KV // 512):
            t0 = t * 512
            inq = (t0 >= Q0) and (t0 < Q0 + Q)
            halves = []
            for h in range(2):
                xt, xtb, xts = xtok_sl.next()
                P.dma(lambda e, xt=xt, t0=t0, h=h: e.dma_start(
                    out=xt, in_=x_rows(t0 + h * 256, 256).rearrange("(b p) d -> p b d", p=128)), xts, writes=[xtb])
                halves.append((xt, xtb))
            if kind == "P":
                vr, vrb, vrs = vr_sl.next()
                P.dma(lambda e, vr=vr, t0=t0: e.dma_start(out=vr, in_=vrow0[:, kv0 + t0:kv0 + t0 + 512]), vrs, writes=[vrb])
            for c in range(8):
                bap, bb = pb_tr.next()

                def f(e, c=c, bap=bap, halves=halves):
                    for b in range(4):
                        ins = e.transpose(out=bap[:, b * 128:(b + 1) * 128], in_=halves[b // 2][0][:, b % 2, c * 128:(c + 1) * 128],
                                          identity=ident)
                    return ins
                P.op("pe", f, reads=[halves[0][1], halves[1][1], B_consts], writes=[bb])
                evac_copy(xT[:, c, :], bap, [bb], [B_x[c]], eng="dve")
            if inq:
                P.dma(lambda e, t0=t0: e.dma_start(out=xT_s[:, :, t0 - Q0:t0 - Q0 + 512].rearrange("c p t -> p c t"), in_=xT),
                      "xTa", reads=[B_x], writes=[B_xTs])
            squares(lambda c: xT[:, c, :], B_x, sq, B_sq)
            rstd, B_rstd = rms_rstd(sq, B_sq, pb_ss.next(), lnt, B_lnt)
            hT, B_h = hT_sl.next()
            norm_affine(xT, B_x, rstd, B_rstd, tmps, hT, B_h, 0, 0, s)
            if t == 0:
                tap("hT", hT, [B_h])
            for g in range(3):
                if g == 0 and not inq:
                    continue
                for m in range(4):
                    bap, bb = pb_mm.next()

                    def f(e, g=g, m=m, bap=bap, hT=hT):
                        for kc in range(8):
                            ins = e.matmul(bap, lhsT=winr[:, g, kc, m * 128:(m + 1) * 128], rhs=hT[:, kc, :], start=(kc == 0), stop=(kc == 7))
                        return ins
                    P.op("pe", f, reads=[B_win, B_h], writes=[bb])
                    if g == 0:
                        evac_copy(qT[:, m, t0 - Q0:t0 - Q0 + 512], bap, [bb], [B_qT])
                    elif g == 1:
                        evac_copy(kT[:, m, t0:t0 + 512], bap, [bb], [B_kT])
                    else:
                        if kind == "P":
                            P.op("dve", lambda e, m=m, bap=bap, t0=t0, vr=vr: e.tensor_tensor(out=vT[:, m, t0:t0 + 512], in0=bap, in1=vr, op=ALU.mult),
                                 reads=[bb, vrb], writes=[B_vT])
                        else:
                            evac_copy(vT[:, m, t0:t0 + 512], bap, [bb], [B_vT])
            for b in range(4):
                tb0 = t0 + b * 128
                ub = (tb0 - (Q0 - 128)) // 128
                if tb0 < Q0 - 128 or tb0 >= Q0 + Q + 128 or tb0 < 0 or tb0 >= KV:
                    continue
                bap, bb = pb_mm.next()

                def f(e, b=b, bap=bap, hT=hT):
                    for kc in range(8):
                        ins = e.matmul(bap, lhsT=hT[:, kc, b * 128:(b + 1) * 128], rhs=winr[:, 3, kc, :], start=(kc == 0), stop=(kc == 7))
                    return ins
                P.op("pe", f, reads=[B_win, B_h], writes=[bb])
                if kind == "P":
                    P.op("dve", lambda e, bap=bap, ub=ub: e.tensor_scalar(out=utok[:, ub, :], in0=bap, scalar1=cf[:, CF_UVALID + idx * 14 + ub:CF_UVALID + idx * 14 + ub + 1],
                                                                       scalar2=None, op0=ALU.mult),
                         reads=[bb, B_consts], writes=[B_u])
                else:
                    evac_copy(utok[:, ub, :], bap, [bb], [B_u])
        P.barrier()
        A.top = m_a
        tap("kT", kT, [B_kT])
        tap("qT", qT, [B_qT])
        tap("vT", vT, [B_vT])
        hi_seg = A.hi
        yaT = A.alloc([4, Q], BF16, high=True)
        ybT = A.alloc([4, Q], BF16, high=True)
        B_ya, B_yb = Buf("yaT"), Buf("ybT")
        m_b = A.top
        band_tile[0] = A.alloc([CBAND_N], BF16)
        B_band = Buf("bands")
        P.dma(lambda e, bt=band_tile[0]: e.dma_start(out=bt, in_=cbands), "bands", writes=[B_band])
        pl_sl = Rot([(A.alloc([512], BF16), Buf(f"pl{i}")) for i in range(2)])
        pb_p = bankpool([0, 1])
        pb_y = bankpool([2, 3])
        nqb = Q // 128
        for tq in range(Q // 512):
            for g in range(4):
                bap, bb = pb_p.next()

                def f(e, g=g, bap=bap, tq=tq, bt=band_tile[0]):
                    def band(base, g_, nb_):
                        o = base + (g_ * 3 + nb_) * 128
                        return bt[:, o:o + 128]
                    for b in range(4):
                        qb = tq * 4 + b
                        base = CB_BINT
                        nbs = [0, 1, 2]
                        if kind == "S":
                            if qb == 0:
                                base, nbs = CB_BFIRST, [1, 2]
                            elif qb == nqb - 1:
                                base, nbs = CB_BLAST, [0, 1]
                        else:
                            if idx == 0 and qb == 2:
                                base = CB_BPS
                            if idx == 2 and qb == 9:
                                base = CB_BPE
                        for i, nb in enumerate(nbs):
                            ins = e.matmul(bap[:, b * 128:(b + 1) * 128], lhsT=utok[:, qb + nb, g * 128:(g + 1) * 128],
                                           rhs=band(base, g, nb), start=(i == 0), stop=(i == len(nbs) - 1))
                    return ins
                P.op("pe", f, reads=[B_u, B_band], writes=[bb])
                pl, plb = pl_sl.next()
                evac_copy(pl, bap, [bb], [plb])
                bap2, bb2 = pb_y.next()
                P.op("pe", lambda e, g=g, bap2=bap2, pl=pl: e.matmul(bap2, lhsT=poolw_b[:, g, :], rhs=pl, start=True, stop=True),
                     reads=[plb, B_consts], writes=[bb2])
                P.op("act", lambda e, g=g, bap2=bap2, tq=tq: e.activation(out=ybT[:, g, tq * 512:(tq + 1) * 512], in_=bap2, func=AF.Identity,
                                                                          scale=PT[:, 200 + g:201 + g]),
                     reads=[bb2, B_PT], writes=[B_yb])
        P.barrier()
        A.top = m_b
        tap("ybT", ybT, [B_yb])
        vblocks = l0_vblocks(KV, Q0, Q)
        NVB = len(vblocks)
        vaug = A.alloc([NVB, 3, 64], BF16)
        B_vaug = Buf("vaug")
        acc_sl = Rot([(A.alloc([Q], F32), Buf(f"acc{i}")) for i in range(2)])
        ebraw_sl = Rot([(A.alloc([3, 256], F32), Buf(f"ebr{i}"), f"ebr{i}") for i in range(2)])
        eb_sl = Rot([(A.alloc([3, 256], BF16), Buf(f"eb{i}")) for i in range(2)])
        e_sl = Rot([(A.alloc([256], BF16), Buf(f"E{i}")) for i in range(4)])
        p_sl = Rot([(A.alloc([256], BF16), Buf(f"Pt{i}")) for i in range(4)])
        lnd = A.alloc([512], F32)
        B_lnd = Buf("lnd")
        pb_s = bankpool([0, 1, 2])
        pb_o = bankpool([3, 4])
        pb_r = bankpool([5])
        pb_v = bankpool([6, 7])
        if kind == "S":
            P.op("pool", lambda e: e.memset(vaug[:, :, 1, :], 1.0), writes=[B_vaug])
        else:
            P.op("dve", lambda e: e.tensor_copy(out=vaug[:, :, 1, :], in_=cbt[:, CB_VONES0 + idx * 61:CB_VONES0 + idx * 61 + NVB].unsqueeze(2).broadcast_to([128, NVB, 64])),
                 reads=[B_consts], writes=[B_vaug])
        for c in range(4):
            pair_items = []
            i = 0
            while i < NVB:
                n = min(8, NVB - i)
                bap, bb = pb_v.next()
                bapb = bap.bitcast(BF16)

                def f(e, i=i, n=n, bapb=bapb, c=c):
                    for k in range(n):
                        br, cr, kb, M = vblocks[i + k]
                        d = DILS[br]
                        st = cr + d * kb
                        ins = e.transpose(out=bapb[0:M, k * 128:(k + 1) * 128], in_=vT[:, c, st:st + d * (M - 1) + 1:d], identity=identb)
                    return ins
                P.op("pe", f, reads=[B_vT, B_consts], writes=[bb])
                Ms = [vblocks[i + k][3] for k in range(n)]
                if all(M == 128 for M in Ms):
                    P.op("dve", lambda e, i=i, n=n, bapb=bapb: e.tensor_copy(
                        out=vaug[:, i:i + n, 0:3:2, :], in_=bapb[:, 0:n * 128].rearrange("p (k t f) -> p k t f", k=n, t=2)),
                        reads=[bb], writes=[B_vaug])
                else:
                    for k in range(n):
                        M = Ms[k]
                        P.op("dve", lambda e, i=i, k=k, M=M, bapb=bapb: e.tensor_copy(
                            out=vaug[0:M, i + k, 0:3:2, :], in_=bapb[0:M, k * 128:(k + 1) * 128].rearrange("p (t f) -> p t f", t=2)),
                            reads=[bb], writes=[B_vaug])
                i += n
            for hh in range(2):
                h = 2 * c + hh
                r0 = hh * 64
                ebr, ebrb, ebrs = ebraw_sl.next()
                eb, ebb = eb_sl.next()
                src = bass.AP(tensor=t5pad_t, offset=h * 384, ap=[[1, 128], [8 * 384, 3], [1, 256]])
                P.dma(lambda e, ebr=ebr, src=src: e.dma_start(out=ebr, in_=src), ebrs, writes=[ebrb])
                ba, bab = pb_v.next()
                P.op("pe", lambda e, ba=ba, ebr=ebr: e.matmul(ba, lhsT=cf[:, CF_J128:CF_J128 + 128], rhs=ebr[:, 0:2, :].rearrange("p a b -> p (a b)"),
                                                            start=True, stop=True), reads=[ebrb, B_consts], writes=[bab])
                P.op("act", lambda e, ba=ba, eb=eb: e.activation(out=eb[:, 0:2, :].rearrange("p a b -> p (a b)"), in_=ba, func=AF.Exp),
                     reads=[bab], writes=[ebb])
                ba2, bab2 = pb_v.next()
                P.op("pe", lambda e, ba2=ba2, ebr=ebr: e.matmul(ba2[:, 0:256], lhsT=cf[:, CF_J128:CF_J128 + 128], rhs=ebr[:, 2, :],
                                                              start=True, stop=True), reads=[ebrb, B_consts], writes=[bab2])
                P.op("act", lambda e, ba2=ba2, eb=eb: e.activation(out=eb[:, 2, :], in_=ba2[:, 0:256], func=AF.Exp),
                     reads=[bab2], writes=[ebb])
                acc, accb = acc_sl.next()
                P.op("pool", lambda e, acc=acc: e.memset(acc, 0.0), writes=[accb])
                o0, d0 = (0, 64) if hh == 0 else (64, 0)
                head_items = []
                for vi, (br, cr, kb, M) in enumerate(vblocks):
                    d = DILS[br]
                    qa, qb_ = Q0 // d, (Q0 + Q) // d
                    qs = max(kb - 64, qa)
                    qe = min(kb + M + 64, qb_)
                    if qe <= qs:
                        continue
                    N = qe - qs
                    off = qs - (kb - 64)
                    kst = cr + d * kb
                    qst = cr + d * qs - Q0
                    ksl = slice(kst, kst + d * (M - 1) + 1, d)
                    qsl = slice(qst, qst + d * (N - 1) + 1, d)

                    def mk(vi=vi, br=br, M=M, N=N, off=off, ksl=ksl, qsl=qsl, r0=r0, c=c, hh=hh, eb=eb, ebb=ebb, acc=acc, accb=accb):
                        st = {}

                        def s1():
                            bs, bsb = pb_s.next()
                            P.op("pe", lambda e: e.matmul(bs[0:M, 0:N], lhsT=kT[r0:r0 + 64, c, ksl], rhs=qT[r0:r0 + 64, c, qsl], start=True, stop=True),
                                 reads=[B_kT, B_qT], writes=[bsb])
                            E, Eb = e_sl.next()
                            P.op("act", lambda e: e.activation(out=E[0:M, 0:N], in_=bs[0:M, 0:N], func=AF.Exp, scale=0.125),
                                 reads=[bsb], writes=[Eb])
                            Pt, Ptb = p_sl.next()
                            P.op("pool", lambda e: e.tensor_tensor(out=Pt[0:M, 0:N], in0=E[0:M, 0:N], in1=eb[0:M, br, off:off + N], op=ALU.mult),
                                 reads=[Eb, ebb], writes=[Ptb])
                            st["Pt"] = (Pt, Ptb)

                        def s2():
                            Pt, Ptb = st["Pt"]
                            bo, bob = pb_o.next()
                            P.op("pe", lambda e: e.matmul(bo[:, 0:N], lhsT=vaug[0:M, vi].rearrange("p a b -> p (a b)")[:, hh * 64:hh * 64 + 128],
                                                          rhs=Pt[0:M, 0:N], start=True, stop=True), reads=[B_vaug, Ptb], writes=[bob])
                            P.op("dve", lambda e: e.tensor_tensor(out=acc[:, qsl], in0=bo[:, 0:N], in1=acc[:, qsl], op=ALU.add),
                                 reads=[bob, accb], writes=[accb])
                        return [s1, s2]
                    head_items.append(mk())

                def fin(acc=acc, accb=accb, o0=o0, d0=d0, c=c):
                    for tq in range(Q // 512):
                        cs = slice(tq * 512, (tq + 1) * 512)
                        P.op("act", lambda e, cs=cs: e.activation(out=lnd[d0:d0 + 64, :], in_=acc[d0:d0 + 64, cs], func=AF.Ln, bias=tinyb[d0:d0 + 64, :]),
                             reads=[accb, B_consts], writes=[B_lnd])
                        rb, rbb = pb_r.next()
                        P.op("act", lambda e, rb=rb: e.activation(out=rb[d0:d0 + 64, :], in_=lnd[d0:d0 + 64, :], func=AF.Exp, scale=-1.0),
                             reads=[B_lnd], writes=[rbb])
                        P.op("dve", lambda e, cs=cs, rb=rb: e.tensor_tensor(
                            out=yaT[o0:o0 + 64, c, cs], in0=acc[o0:o0 + 64, cs], in1=rb[d0:d0 + 64, :], op=ALU.mult),
                            reads=[accb, rbb], writes=[B_ya])
                last_s2 = head_items[-1][1]

                def s2_fin(last_s2=last_s2, fin=fin):
                    last_s2()
                    fin()
                head_items[-1][1] = s2_fin
                pair_items.extend(head_items)
            run_pipeline(pair_items, 3)
        P.barrier()
        tap("yaT", yaT, [B_ya])
        A.top = m_seg
        B_mix = Buf("mix")

        def mix_chunks(kc, t0, n):
            return yaT[:, kc, t0:t0 + n] if kc < 4 else ybT[:, kc - 4, t0:t0 + n]

        def x_src(t0):
            return xT_s[:, :, t0:t0 + 512].rearrange("c p t -> p c t")

        def sink(what, t0, xT_, B_x_, ctx=None):
            if what == "alloc":
                return dict(h1=Rot([(A.alloc([8, 512], BF16), Buf(f"h1{i}"), f"h1{i}") for i in range(1)]))
            if what == "q1":
                P.dma(lambda e: e.dma_start(out=x1_dst[:, :, dst_off + t0:dst_off + t0 + 512].rearrange("c p t -> p c t"), in_=xT_),
                      "x1st", reads=[B_x_])
                if t0 == 0:
                    tap("x1T", xT_, [B_x_])
                squares(lambda c: xT_[:, c, :], B_x_, ctx["sqB"], ctx["B_sqB"])
                return
            if what == "q2a":
                ctx["rH"] = rms_rstd(ctx["sqB"], ctx["B_sqB"], ctx["ssB"], ctx["lntB"], ctx["B_lntB"])
                return
            rstd_, B_rstd_ = ctx["rH"]
            h1, h1b, h1s = ctx["extra"]["h1"].next()
            norm_affine(xT_, B_x_, rstd_, B_rstd_, ctx["tmps"], h1, h1b, 1, 0, s)
            P.dma(lambda e: e.dma_start(out=h1_dst[:, :, dst_off + t0:dst_off + t0 + 512].rearrange("c p t -> p c t"), in_=h1),
                  h1s, reads=[h1b])

        B_mix.w = None
        P.barrier()
        phase_d(0, s, Q, mix_chunks, B_mix, x_src, B_xTs, sink)
        A.top = m_seg
        A.hi = hi_seg

    def layer1_segment(kind, idx):
        if kind == "S":
            KV, Q0, Q = SEQ_S, 0, SEQ_S
            s = idx
            h1_src = h1s_s[idx]
            x1_src = x1s_s[idx]
            src_off = 0
            y_dst = ys[idx]
            y_off = 0
        else:
            KV, Q0, Q = 2560, 256, 2048
            s = 4
            h1_src = h1p_s
            x1_src = x1p_s
            src_off = 2048 * idx
            y_dst = yp
            y_off = 2048 * idx
        m_seg = A.top
        kT = A.alloc([8, KV], BF16)
        vT = A.alloc([8, KV], BF16)
        qT = A.alloc([8, Q], BF16)
        B_kT, B_vT, B_qT = Buf("kT1"), Buf("vT1"), Buf("qT1")
        B_oT = Buf("oT")
        m_a = A.top
        h1_sl = Rot([(A.alloc([8, 512], BF16), Buf(f"h1l{i}"), f"h1l{i}") for i in range(2)])
        wq_sl = Rot([(A.alloc([8, 512], BF16), Buf(f"wq{i}"), f"wq{i}") for i in range(3)])
        vr_sl = Rot([(A.alloc([512], BF16), Buf(f"vr1{i}"), f"vr1{i}") for i in range(2)])
        pb_mm = bankpool([0, 1, 2, 3, 4, 5, 6, 7])
        for t in range(KV // 512):
            t0 = t * 512
            inq = (t0 >= Q0) and (t0 < Q0 + Q)
            h1, h1b, h1sem = h1_sl.next()
            P.dma(lambda e, h1=h1, t0=t0: e.dma_start(out=h1, in_=h1_src[:, :, src_off + t0:src_off + t0 + 512].rearrange("c p t -> p c t")),
                  h1sem, writes=[h1b])
            if kind == "P":
                vr, vrb, vrs = vr_sl.next()
                P.dma(lambda e, vr=vr, t0=t0: e.dma_start(out=vr, in_=vrow0[:, 1024 + src_off + t0:1024 + src_off + t0 + 512]), vrs, writes=[vrb])
            qlo, qhi = max(t0, Q0), min(t0 + 512, Q0 + Q)
            for g in range(6):
                if g < 2 and qhi <= qlo:
                    continue
                wq, wqb, wqs = wq_sl.next()
                P.dma(lambda e, wq=wq, g=g: e.dma_start(out=wq.rearrange("p a b -> p (a b)"), in_=wqkv_s[g]), wqs, writes=[wqb])
                for mm in range(4):
                    m = (g % 2) * 4 + mm
                    bap, bb = pb_mm.next()
                    c0, c1 = (qlo - t0, qhi - t0) if g < 2 else (0, 512)

                    def f(e, wq=wq, mm=mm, bap=bap, h1=h1, c0=c0, c1=c1):
                        for kc in range(8):
                            ins = e.matmul(bap[:, 0:c1 - c0], lhsT=wq[:, kc, mm * 128:(mm + 1) * 128], rhs=h1[:, kc, c0:c1], start=(kc == 0), stop=(kc == 7))
                        return ins
                    P.op("pe", f, reads=[wqb, h1b], writes=[bb])
                    if g < 2:
                        evac_copy(qT[:, m, qlo - Q0:qhi - Q0], bap[:, 0:c1 - c0], [bb], [B_qT])
                    elif g < 4:
                        evac_copy(kT[:, m, t0:t0 + 512], bap, [bb], [B_kT])
                    else:
                        if kind == "P":
                            P.op("dve", lambda e, m=m, bap=bap, t0=t0, vr=vr: e.tensor_tensor(out=vT[:, m, t0:t0 + 512], in0=bap, in1=vr, op=ALU.mult),
                                 reads=[bb, vrb], writes=[B_vT])
                        else:
                            evac_copy(vT[:, m, t0:t0 + 512], bap, [bb], [B_vT])
        P.barrier()
        A.top = m_a
        tap("kT1", kT, [B_kT])
        tap("qT1", qT, [B_qT])
        hi_seg = A.hi
        oT = A.alloc([8, Q], BF16, high=True)
        NB = KV // 128
        NQB = Q // 128
        vaug = A.alloc([NB, 3, 64], BF16)
        B_vaug = Buf("vaug1")
        traw_sl = Rot([(A.alloc([14, 64], F32), Buf(f"traw{i}"), f"traw{i}") for i in range(2)])
        texp_sl = Rot([(A.alloc([14, 64], F32), Buf(f"texp{i}")) for i in range(2)])
        tt_sl = Rot([(A.alloc([4, 14, 64], BF16), Buf(f"tt{i}")) for i in range(2)])
        e_sl = Rot([(A.alloc([512], BF16), Buf(f"E1{i}")) for i in range(3)])
        p_sl = Rot([(A.alloc([512], BF16), Buf(f"P1{i}")) for i in range(4)])
        lnd = A.alloc([512], F32)
        B_lnd = Buf("lnd1")

        pb_s = bankpool([0, 1, 2, 3])
        pb_o = bankpool([4, 5, 6])
        pb_v = bankpool([7])
        if kind == "S":
            P.op("pool", lambda e: e.memset(vaug[:, :, 1, :], 1.0), writes=[B_vaug])
        else:
            P.op("dve", lambda e: e.tensor_copy(out=vaug[:, :, 1, :], in_=cbt[:, CB_VONES1 + idx * 20:CB_VONES1 + idx * 20 + NB].unsqueeze(2).broadcast_to([128, NB, 64])),
                 reads=[B_consts], writes=[B_vaug])
        for c in range(8):
            pair_items = []
            i = 0
            while i < NB:
                n = min(8, NB - i)
                bap, bb = pb_v.next()
                bapb = bap.bitcast(BF16)

                def f(e, i=i, n=n, bapb=bapb, c=c):
                    for k in range(n):
                        ins = e.transpose(out=bapb[:, k * 128:(k + 1) * 128], in_=vT[:, c, (i + k) * 128:(i + k + 1) * 128], identity=identb)
                    return ins
                P.op("pe", f, reads=[B_vT, B_consts], writes=[bb])
                P.op("dve", lambda e, i=i, n=n, bapb=bapb: e.tensor_copy(
                    out=vaug[:, i:i + n, 0:3:2, :], in_=bapb[:, 0:n * 128].rearrange("p (k t f) -> p k t f", k=n, t=2)),
                    reads=[bb], writes=[B_vaug])
                i += n
            for hh in range(2):
                h = 2 * c + hh
                r0 = hh * 64
                o0, d0 = (0, 64) if hh == 0 else (64, 0)
                traw, trawb, traws = traw_sl.next()
                texp, B_texp = texp_sl.next()
                for half in range(2):
                    src = bass.AP(tensor=rpbpad_t, offset=(h * 15 + (1 - half)) * 128, ap=[[1, 64], [128, 14], [1, 64]])
                    P.dma(lambda e, traw=traw, src=src, half=half: e.dma_start(out=traw[half * 64:(half + 1) * 64], in_=src), traws, writes=[trawb])
                for hf7 in range(2):
                    ba, bab = pb_v.next()
                    P.op("pe", lambda e, ba=ba, traw=traw, hf7=hf7: e.matmul(
                        ba[:, 0:448], lhsT=cf[:, CF_J2:CF_J2 + 128], rhs=traw[:, hf7 * 7:(hf7 + 1) * 7, :].rearrange("p a b -> p (a b)"),
                        start=True, stop=True), reads=[trawb, B_consts], writes=[bab])
                    P.op("act", lambda e, ba=ba, hf7=hf7, texp=texp: e.activation(out=texp[:, hf7 * 7:(hf7 + 1) * 7, :].rearrange("p a b -> p (a b)"),
                                                                       in_=ba[:, 0:448], func=AF.Exp), reads=[bab], writes=[B_texp])
                P.op("dve", lambda e, texp=texp: e.tensor_tensor(out=texp, in0=texp, in1=colmask.unsqueeze(1).broadcast_to([128, 14, 64]), op=ALU.mult),
                     reads=[B_texp, B_consts], writes=[B_texp])
                tt, ttb = tt_sl.next()
                P.op("dve", lambda e, tt=tt, texp=texp: e.tensor_copy(out=tt[:, 0], in_=texp), reads=[B_texp], writes=[ttb])
                for ti, mo in ((1, CF_MINT), (2, CF_MSTART), (3, CF_MEND)):
                    if kind == "S" and ti > 1:
                        continue
                    P.op("dve", lambda e, tt=tt, ti=ti, mo=mo, texp=texp: e.tensor_tensor(
                        out=tt[:, ti], in0=texp, in1=cf[:, mo:mo + 14].unsqueeze(2).broadcast_to([128, 14, 64]), op=ALU.mult),
                        reads=[B_texp, B_consts], writes=[ttb])
                def blk_info(qb):
                    kb0 = (Q0 // 128) + qb
                    if kind == "S":
                        if qb == 0:
                            js, ti = [2, 3, 4, 5], 0
                        elif qb == 1:
                            js, ti = [1, 2, 3, 4], 0
                        elif qb == NQB - 2:
                            js, ti = [0, 1, 2, 3], 0
                        elif qb == NQB - 1:
                            js, ti = [-1, 0, 1, 2], 0
                        else:
                            js, ti = [0, 1, 2, 3, 4], 1
                    else:
                        js, ti = [0, 1, 2, 3, 4], 1
                        if idx == 0 and qb == 0:
                            js, ti = [0, 1, 2, 3, 4, 5], 2
                        elif idx == 0 and qb == 1:
                            ti = 2
                        elif idx == 1 and qb == NQB - 2:
                            ti = 3
                        elif idx == 1 and qb == NQB - 1:
                            js, ti = [-1, 0, 1, 2, 3, 4], 3
                    return [kb0 - 2 + j for j in js], ti
                infos = [blk_info(qb) for qb in range(NQB)]
                head_items = []
                tile_states = {}
                for kblk in range(NB):
                    qbs = [qb for qb in range(NQB) if kblk in infos[qb][0]]
                    if not qbs:
                        continue
                    assert qbs == list(range(qbs[0], qbs[-1] + 1))
                    for qt in sorted(set(qb // 4 for qb in qbs)):
                        blks = [qb for qb in qbs if qb // 4 == qt]
                        qb0, nbk = blks[0], len(blks)
                        N = 128 * nbk
                        c0 = (qb0 % 4) * 128
                        runs = []
                        for qb in blks:
                            ti = infos[qb][1]
                            if runs and runs[-1][2] == ti:
                                runs[-1][1] += 1
                            else:
                                ra = 2 * ((Q0 // 128) + qb)
                                ee0 = ra - 2 * kblk + 6
                                runs.append([qb - qb0, 1, ti, ee0])
                        for (_, nb_, _, ee0_) in runs:
                            assert 0 <= ee0_ and ee0_ + 2 * nb_ <= 14, (kind, idx, kblk, qt, runs)
                        ts = tile_states.setdefault(qt, dict(n=0))
                        ts["n"] += 1

                        def mk(kblk=kblk, qb0=qb0, N=N, c0=c0, runs=runs, r0=r0, c=c, hh=hh, tt=tt, ttb=ttb, ts=ts, qt=qt, o0=o0, d0=d0):
                            st = {}

                            def s1():
                                bs, bsb = pb_s.next()
                                P.op("pe", lambda e: e.matmul(bs[:, 0:N], lhsT=kT[r0:r0 + 64, c, kblk * 128:(kblk + 1) * 128],
                                                              rhs=qT[r0:r0 + 64, c, qb0 * 128:qb0 * 128 + N], start=True, stop=True),
                                     reads=[B_kT, B_qT], writes=[bsb])
                                E, Eb = e_sl.next()
                                P.op("act", lambda e: e.activation(out=E[:, 0:N], in_=bs[:, 0:N], func=AF.Exp, scale=0.125), reads=[bsb], writes=[Eb])
                                Pt, Ptb = p_sl.next()
                                for (ob, nb_, ti, ee0) in runs:
                                    P.op("dve", lambda e, ob=ob, nb_=nb_, ti=ti, ee0=ee0: e.tensor_tensor(
                                        out=Pt[:, ob * 128:(ob + nb_) * 128].rearrange("p (a b) -> p a b", b=64),
                                        in0=E[:, ob * 128:(ob + nb_) * 128].rearrange("p (a b) -> p a b", b=64),
                                        in1=tt[:, ti, ee0:ee0 + 2 * nb_, :], op=ALU.mult), reads=[Eb, ttb], writes=[Ptb])
                                st["Pt"] = (Pt, Ptb)

                            def s2():
                                Pt, Ptb = st["Pt"]
                                first = "bo" not in ts
                                if first:
                                    ts["bo"] = pb_o.next()
                                    ts["done"] = 0
                                bo, bob = ts["bo"]
                                ts["done"] += 1
                                last = ts["done"] == ts["n"]
                                P.op("pe", lambda e: e.matmul(bo[:, c0:c0 + N],
                                                              lhsT=vaug[:, kblk].rearrange("p a b -> p (a b)")[:, hh * 64:hh * 64 + 128], rhs=Pt[:, 0:N],
                                                              start=first, stop=last, skip_group_check=True), reads=[B_vaug, Ptb], writes=[bob])
                                if last:
                                    cs = slice(qt * 512, (qt + 1) * 512)
                                    P.op("act", lambda e: e.activation(out=lnd[d0:d0 + 64, :], in_=bo[d0:d0 + 64, :], func=AF.Ln, bias=tinyb[d0:d0 + 64, :]),
                                         reads=[bob, B_consts], writes=[B_lnd])
                                    P.op("act", lambda e: e.activation(out=lnd[d0:d0 + 64, :], in_=lnd[d0:d0 + 64, :], func=AF.Exp, scale=-1.0),
                                         reads=[B_lnd], writes=[B_lnd])
                                    P.op("dve", lambda e: e.tensor_tensor(out=oT[o0:o0 + 64, c, cs], in0=bo[o0:o0 + 64, :], in1=lnd[d0:d0 + 64, :], op=ALU.mult),
                                         reads=[bob, B_lnd], writes=[B_oT])
                            return [s1, s2]
                        head_items.append(mk())
                pair_items.extend(head_items)
            run_pipeline(pair_items, 3)
        P.barrier()
        tap("oT", oT, [B_oT])
        A.top = m_seg
        B_x1 = Buf("x1src")

        def mix_chunks(kc, t0, n):
            return oT[:, kc, t0:t0 + n]

        def x_src(t0):
            o = src_off + Q0 + t0
            return x1_src[:, :, o:o + 512].rearrange("c p t -> p c t")

        def sink(what, t0, xT_, B_x_, ctx=None):
            if what == "alloc":
                return dict()
            if what in ("q2a", "q2b"):
                return
            xo = ctx["yTd"].rearrange("p a b -> p (a b)").rearrange("p (b d) -> p b d", b=4)
            xob = ctx["B_yTd"]
            for b in range(4):
                for hf in range(2):
                    bap, bb = ctx["pb_mm"].next()

                    def f(e, b=b, hf=hf, bap=bap):
                        for cc in range(4):
                            ins = e.transpose(out=bap[:, cc * 128:(cc + 1) * 128], in_=xT_[:, hf * 4 + cc, b * 128:(b + 1) * 128], identity=ident)
                        return ins
                    P.op("pe", f, reads=[B_x_, B_consts], writes=[bb])
                    evac_copy(xo[:, b, hf * 512:(hf + 1) * 512], bap, [bb], [xob])
            P.dma(lambda e: e.dma_start(out=y_dst[y_off + t0:y_off + t0 + 512, :].rearrange("(b p) d -> p b d", p=128), in_=xo),
                  "xo", reads=[xob], writes=[xob])

        phase_d(1, s, Q, mix_chunks, B_oT, x_src, B_x1, sink)
        A.top = m_seg
        A.hi = hi_seg

    for si in range(n_sample):
        layer0_segment("S", si)
    if do_prompt:
        for m in range(3):
            layer0_segment("P", m)
    if "x1s_full" in tap_out:
        P.barrier()
        P.dma(lambda e: e.dma_start(out=tap_out["x1s_full"], in_=x1s_s[0]), "tap")
        P.dma(lambda e: e.dma_start(out=tap_out["h1s_full"], in_=h1s_s[0]), "tap")
        P.barrier()
    if do_layer1:
        for si in range(n_sample):
            layer1_segment("S", si)
        if do_prompt:
            for n in range(2):
                layer1_segment("P", n)
    P.final_wait("sp")
    P.emit(nc)
    es.close()
    print(f"[kernel] ops={P.nops} arena_peak={A.peak}")
    return nc


def make_in_maps(x_prompt, x_sample, c_prompt, c_sample, norm_g, ada_w, ada_b, ffn_w_gate, ffn_w_up,
                 ffn_w_down, even_w_in, even_w_out, pool_w, pool_scale, t5_table, odd_w_qkv, odd_w_out, odd_rpb):
    f32 = np.float32
    x_prompt = np.asarray(x_prompt, f32)
    x_sample = np.asarray(x_sample, f32)
    c_prompt = np.asarray(c_prompt, f32)
    c_sample = np.asarray(c_sample, f32)
    norm_g = np.asarray(norm_g, f32)
    ada_b = np.asarray(ada_b, f32)
    t5_table = np.asarray(t5_table, f32)
    odd_rpb = np.asarray(odd_rpb, f32)
    t5pad = np.full((3, 8, 384), -30000.0, f32)
    for br, d in enumerate(DILS):
        rel = np.arange(-64, 65)
        bias = t5_table[_t5_bucket(rel * d)]
        for m in range(129):
            t5pad[br, :, 127 + m] = bias[128 - m]
    rpbpad = np.zeros((16, 15, 128), f32)
    rpbpad[:, :, 48:79] = odd_rpb[0][:, ::-1, ::-1]
    shared = dict(
        adaw=np.asarray(ada_w, f32), wgate=np.asarray(ffn_w_gate, f32), wup=np.asarray(ffn_w_up, f32),
        wdown=np.asarray(ffn_w_down, f32), win=np.asarray(even_w_in, f32)[0], wout0=np.asarray(even_w_out, f32)[0],
        poolw=np.asarray(pool_w, f32)[0], wqkv=np.asarray(odd_w_qkv, f32)[0], wout1=np.asarray(odd_w_out, f32)[0],
        t5pad=t5pad.reshape(-1), rpbpad=rpbpad.reshape(-1))
    in_maps = []
    for core in range(8):
        p, b = core % 4, core // 4
        xpb = np.zeros((PBUF, D), f32)
        lo = 4096 * p - HALO0
        hi = lo + PBUF
        slo, shi = max(lo, 0), min(hi, 16384)
        xpb[slo - lo:shi - lo] = x_prompt[b, slo:shi]
        cvec = np.concatenate([c_sample[4 * core:4 * core + 4], c_prompt[b:b + 1]], 0)
        prm = np.zeros((256, 128), f32)
        prm[0:40] = cvec.reshape(5, 8, 128).transpose(1, 0, 2).reshape(40, 128)
        prm[40:104] = norm_g.reshape(64, 128)
        prm[104:200] = ada_b.reshape(96, 128)
        prm[200:204] = np.asarray(pool_scale, f32)[0].reshape(4, 128)
        cf, cb, vrow, cbd = host_consts(core)
        m = dict(shared)
        m.update(xs=np.ascontiguousarray(x_sample[4 * core:4 * core + 4]), xp=xpb, prm=prm, cstf=cf, cstb=cb, vrow0=vrow, cbands=cbd)
        in_maps.append(m)
    return in_maps


_NC_CACHE = {}


def kernel(**inputs):
    in_maps = make_in_maps(**inputs)
    if "nc" not in _NC_CACHE:
        _NC_CACHE["nc"] = build_program()
    nc = _NC_CACHE["nc"]
    res = run_bass_kernel_spmd(nc, in_maps, core_ids=list(range(8)))
    y_prompt = np.zeros((2, 16384, D), np.float32)
    y_sample = np.zeros((32, SEQ_S, D), np.float32)
    for core in range(8):
        r = res.results[core]
        p, b = core % 4, core // 4
        y_prompt[b, 4096 * p:4096 * (p + 1)] = r["yp"]
        y_sample[4 * core:4 * core + 4] = r["ys"]
    return (y_prompt, y_sample)
```
